# Optimizing a Trainium2 kernel written in Bass

```python
import math
import jax, jax.numpy as jnp
from jax import lax
import numpy as np

D_MODEL = 1024
BATCH = 2
SEQ = 8192
DEPTH = 2

MIX_WIDTH = D_MODEL
DIFF_HEADS = 4
DIFF_QK_DIM = 64
DIFF_V_DIM = 2 * DIFF_QK_DIM
A_QK = DIFF_HEADS * 2 * DIFF_QK_DIM
A_WIDTH = DIFF_HEADS * DIFF_V_DIM
Q_BLOCK = 128
B_WIDTH = MIX_WIDTH - A_WIDTH
CONV_GROUPS = 8
CONV_WIDTH = 3
EVEN_IN = 2 * A_QK + A_WIDTH + 3 * B_WIDTH
EVEN_SPLITS = (A_QK, 2 * A_QK, 2 * A_QK + A_WIDTH,
               2 * A_QK + A_WIDTH + B_WIDTH, 2 * A_QK + A_WIDTH + 2 * B_WIDTH)
CHUNK = 128
SGU_WIDTH = MIX_WIDTH
SGU_GROUPS = 8
SGU_GROUP_DIM = SGU_WIDTH // SGU_GROUPS
ODD_IN = 2 * SGU_WIDTH
REL_BUCKETS = 32
REL_MAX_DIST = 128
FFN_HIDDEN = -(-8 * D_MODEL // (3 * 256)) * 256

N_EVEN = (DEPTH + 1) // 2
N_ODD = DEPTH // 2
RMS_EPS = 1e-6

kernel_name = "hybrid_diffattn_shortconv_sgu_block"


def rms_norm(x, g, eps=RMS_EPS):
    xf = x.astype(jnp.float32)
    y = xf * lax.rsqrt(jnp.mean(xf * xf, axis=-1, keepdims=True) + eps)
    return (y * g.astype(jnp.float32)).astype(x.dtype)


def layer_norm(x, g, b, eps=1e-5):
    xf = x.astype(jnp.float32)
    mu = jnp.mean(xf, axis=-1, keepdims=True)
    xc = xf - mu
    y = xc * lax.rsqrt(jnp.mean(xc * xc, axis=-1, keepdims=True) + eps)
    return (y * g.astype(jnp.float32) + b.astype(jnp.float32)).astype(x.dtype)


def t5_bucket(q_pos, k_pos):
    n = jnp.maximum(q_pos[:, None] - k_pos[None, :], 0)
    max_exact = REL_BUCKETS // 2
    nf = jnp.maximum(n, 1).astype(jnp.float32)
    large = max_exact + (jnp.log(nf / max_exact) / math.log(REL_MAX_DIST / max_exact)
                         * (REL_BUCKETS - max_exact)).astype(jnp.int32)
    large = jnp.minimum(large, REL_BUCKETS - 1)
    return jnp.where(n < max_exact, n, large)


def diff_attention(q, k, v, rel_bias, lam, subln_g, lambda_init):
    bsz, s = q.shape[0], q.shape[1]
    nblk = s // Q_BLOCK
    lam = lam.astype(jnp.float32)
    lam_full = (jnp.exp(jnp.sum(lam[0] * lam[1])) - jnp.exp(jnp.sum(lam[2] * lam[3]))
                + lambda_init)
    scale = DIFF_QK_DIM ** -0.5
    k_pos = jnp.arange(s, dtype=jnp.int32)
    table = rel_bias.astype(jnp.float32)
    qb = q.reshape(bsz, nblk, Q_BLOCK, DIFF_HEADS, 2, DIFF_QK_DIM)
    qb = jnp.moveaxis(qb, 1, 0)
    starts = jnp.arange(nblk, dtype=jnp.int32) * Q_BLOCK

    def block(args):
        q_blk, start = args
        q_pos = start + jnp.arange(Q_BLOCK, dtype=jnp.int32)
        logits = jnp.einsum('bqhmd,bkhmd->bhmqk', q_blk, k).astype(jnp.float32) * scale
        bias = jnp.transpose(table[t5_bucket(q_pos, k_pos)], (2, 0, 1))
        logits = logits + bias[:, None]
        causal = q_pos[:, None] >= k_pos[None, :]
        logits = jnp.where(causal, logits, -jnp.inf)
        p = jax.nn.softmax(logits, axis=-1)
        attn = p[:, :, 0] - lam_full * p[:, :, 1]
        return jnp.einsum('bhqk,bkhe->bqhe', attn.astype(v.dtype), v)

    o = lax.map(block, (qb, starts))
    o = jnp.moveaxis(o, 0, 1).reshape(bsz, s, DIFF_HEADS, DIFF_V_DIM)
    o = rms_norm(o, subln_g, eps=1e-5) * (1.0 - lambda_init)
    return o.reshape(bsz, s, A_WIDTH)


def short_gated_conv(b_gate, c_gate, h, conv_w):
    z = c_gate * h
    y = lax.conv_general_dilated(
        z, conv_w[:, None, :].astype(z.dtype), window_strides=(1,),
        padding=((CONV_WIDTH - 1, 0),), dimension_numbers=('NWC', 'WIO', 'NWC'),
        feature_group_count=z.shape[-1])
    return b_gate * y


def spatial_gating(xn, w_in, ln_g, ln_b, sgu_w, sgu_b, w_out):
    z = jax.nn.gelu(xn @ w_in, approximate=False)
    u, v = jnp.split(z, 2, axis=-1)
    v = layer_norm(v, ln_g, ln_b)
    bsz, s = v.shape[0], v.shape[1]
    vc = v.reshape(bsz, s // CHUNK, CHUNK, SGU_GROUPS, SGU_GROUP_DIM)
    mask = jnp.tril(jnp.ones((CHUNK, CHUNK), dtype=bool))
    ws = jnp.where(mask[None], sgu_w, jnp.zeros_like(sgu_w))
    mixed = jnp.einsum('gts,bnsgc->bntgc', ws.astype(v.dtype), vc)
    mixed = mixed + jnp.transpose(sgu_b)[:, :, None].astype(v.dtype)
    mixed = mixed.reshape(bsz, s, SGU_WIDTH)
    return (u * mixed) @ w_out


def setup_inputs(seed: int = 0) -> dict:
    key = jax.random.key(seed)
    ks = jax.random.split(key, 20)
    n = jax.random.normal
    f32 = jnp.float32
    return {
        "x": n(ks[0], (BATCH, SEQ, D_MODEL), f32),
        "rel_bias": 0.5 * n(ks[1], (REL_BUCKETS, DIFF_HEADS), f32),
        "w_in_even": n(ks[2], (N_EVEN, D_MODEL, EVEN_IN), f32) * D_MODEL ** -0.5,
        "diff_lambda": 0.1 * n(ks[3], (N_EVEN, 4, DIFF_QK_DIM), f32),
        "diff_subln_g": 1.0 + 0.02 * n(ks[4], (N_EVEN, DIFF_V_DIM), f32),
        "conv_w": n(ks[5], (N_EVEN, CONV_WIDTH, B_WIDTH), f32) * CONV_WIDTH ** -0.5,
        "w_out_even": n(ks[6], (N_EVEN, MIX_WIDTH, D_MODEL), f32) * MIX_WIDTH ** -0.5,
        "w_in_odd": n(ks[7], (N_ODD, D_MODEL, ODD_IN), f32) * D_MODEL ** -0.5,
        "sgu_ln_g": 1.0 + 0.02 * n(ks[8], (N_ODD, SGU_WIDTH), f32),
        "sgu_ln_b": 0.02 * n(ks[9], (N_ODD, SGU_WIDTH), f32),
        "sgu_w": n(ks[10], (N_ODD, SGU_GROUPS, CHUNK, CHUNK), f32) * CHUNK ** -0.5,
        "sgu_b": 1.0 + 0.02 * n(ks[11], (N_ODD, SGU_GROUPS, CHUNK), f32),
        "w_out_odd": n(ks[12], (N_ODD, SGU_WIDTH, D_MODEL), f32) * SGU_WIDTH ** -0.5,
        "norm_g": 1.0 + 0.02 * n(ks[13], (DEPTH, 4, D_MODEL), f32),
        "w_gate": n(ks[14], (DEPTH, D_MODEL, FFN_HIDDEN), f32) * D_MODEL ** -0.5,
        "w_up": n(ks[15], (DEPTH, D_MODEL, FFN_HIDDEN), f32) * D_MODEL ** -0.5,
        "w_down": n(ks[16], (DEPTH, FFN_HIDDEN, D_MODEL), f32) * FFN_HIDDEN ** -0.5,
    }


def reference(x, rel_bias, w_in_even, diff_lambda, diff_subln_g, conv_w, w_out_even,
              w_in_odd, sgu_ln_g, sgu_ln_b, sgu_w, sgu_b, w_out_odd, norm_g,
              w_gate, w_up, w_down):
    bsz, s = x.shape[0], x.shape[1]
    for i in range(DEPTH):
        j = i // 2
        h = rms_norm(x, norm_g[i, 0])
        if i % 2 == 0:
            lambda_init = 0.8 - 0.6 * math.exp(-0.3 * i)
            p = h @ w_in_even[j]
            q, k, v, b_gate, c_gate, hc = jnp.split(p, EVEN_SPLITS, axis=-1)
            q = q.reshape(bsz, s, DIFF_HEADS, 2, DIFF_QK_DIM)
            k = k.reshape(bsz, s, DIFF_HEADS, 2, DIFF_QK_DIM)
            v = v.reshape(bsz, s, DIFF_HEADS, DIFF_V_DIM)
            a_out = diff_attention(q, k, v, rel_bias, diff_lambda[j], diff_subln_g[j],
                                   lambda_init)
            b_out = short_gated_conv(b_gate, c_gate, hc, conv_w[j])
            mix = jnp.concatenate([a_out, b_out], axis=-1) @ w_out_even[j]
        else:
            mix = spatial_gating(h, w_in_odd[j], sgu_ln_g[j], sgu_ln_b[j], sgu_w[j],
                                 sgu_b[j], w_out_odd[j])
        x = x + rms_norm(mix, norm_g[i, 1])
        h = rms_norm(x, norm_g[i, 2])
        f = (jax.nn.silu(h @ w_gate[i]) * (h @ w_up[i])) @ w_down[i]
        x = x + rms_norm(f, norm_g[i, 3])
    return x
```

```python
import numpy as np
import ml_dtypes
from contextlib import ExitStack
import concourse.bass as bass
import concourse.mybir as mybir
from concourse.bass_utils import run_bass_kernel_spmd

F32 = mybir.dt.float32
BF16 = mybir.dt.bfloat16
AF = mybir.ActivationFunctionType
ALU = mybir.AluOpType
AX = mybir.AxisListType
NPBF = ml_dtypes.bfloat16

D = 1024
NCORE = 8
TOK = 2048
SEQ = 8192
FF = 2816
NFB = FF // 128
NEG = -30000.0


class Buf:
    __slots__ = ("w", "rs", "name")

    def __init__(self, name=""):
        self.w = None
        self.rs = {}
        self.name = name


class EngS:
    def __init__(self, name):
        self.name = name
        self.sem = None
        self.count = 0
        self.waited = {}
        self.acts = []
        self.dsems = []
        self.dvals = []
        self.dnext = 0


class KB:
    NDS = 6

    def __init__(self, nc, es):
        self.nc = nc
        self.owner = {}
        self.E = {}
        for n in ("pe", "act", "dve", "pool", "sp"):
            e = EngS(n)
            e.sem = es.enter_context(nc.semaphore("s_" + n))
            self.owner[id(e.sem)] = (e, None)
            self.E[n] = e
        for n in ("sp", "pool", "act"):
            e = self.E[n]
            for i in range(self.NDS):
                s = es.enter_context(nc.semaphore("d_%s%d" % (n, i)))
                e.dsems.append(s)
                e.dvals.append(0)
                self.owner[id(s)] = (e, i)
        self.PE, self.ACT, self.DVE, self.POOL, self.SP = (self.E[n] for n in ("pe", "act", "dve", "pool", "sp"))

    def _avail(self, sem):
        e, i = self.owner[id(sem)]
        return e.count if i is None else e.dvals[i]

    def _wait(self, E, sem, val):
        if sem is E.sem and val > E.count:
            return
        if E.waited.get(id(sem), 0) >= val:
            return
        assert self._avail(sem) >= val, "wait on a value never signalled (%s)" % E.name
        E.waited[id(sem)] = val
        E.acts.append(("wait", sem, val))

    def _deps(self, E, reads, writes):
        deps = {}

        def add(tok):
            if tok is None:
                return
            s, v = tok
            k = id(s)
            if k not in deps or deps[k][1] < v:
                deps[k] = (s, v)
        for b in reads:
            add(b.w)
        for b in writes:
            add(b.w)
            for t in b.rs.values():
                add(t)
        for s, v in deps.values():
            if E is self.PE and s is E.sem:
                continue
            self._wait(E, s, v)

    def _mark(self, tok, reads, writes):
        for b in writes:
            b.w = tok
            b.rs = {}
        for b in reads:
            k = id(tok[0])
            if k not in b.rs or b.rs[k][1] < tok[1]:
                b.rs[k] = tok

    def op(self, E, fn, reads=(), writes=(), signal=True):
        self._deps(E, reads, writes)
        if signal:
            E.count += 1
            E.acts.append(("inst", fn, E.sem, 1))
            tok = (E.sem, E.count)
        else:
            E.acts.append(("inst", fn, None, 0))
            tok = (E.sem, E.count + 1)
        self._mark(tok, reads, writes)

    def dma(self, Q, out, in_, reads=(), writes=(), **kw):
        i = Q.dnext
        Q.dnext = (i + 1) % self.NDS
        s = Q.dsems[i]
        if Q.dvals[i] > 0:
            self._wait(Q, s, Q.dvals[i])
        self._deps(Q, reads, writes)
        Q.dvals[i] += 16
        Q.acts.append(("inst", lambda e: e.dma_start(out=out, in_=in_, **kw), s, 16))
        self._mark((s, Q.dvals[i]), reads, writes)

    def flush(self, final=False):
        nc = self.nc
        if final:
            for n in ("sp", "pool", "act"):
                q = self.E[n]
                for i, s in enumerate(q.dsems):
                    if q.dvals[i] > 0:
                        self._wait(self.SP, s, q.dvals[i])
        with nc.Block() as block:
            def mk(E):
                acts = E.acts

                def body(eng):
                    for a in acts:
                        if a[0] == "wait":
                            eng.wait_ge(a[1], a[2])
                        else:
                            inst = a[1](eng)
                            if a[2] is not None:
                                inst.then_inc(a[2], a[3])
                return body
            block.tensor(mk(self.PE))
            block.scalar(mk(self.ACT))
            block.vector(mk(self.DVE))
            block.gpsimd(mk(self.POOL))
            block.sync(mk(self.SP))
        for e in self.E.values():
            e.acts = []

    def mm(self, out, lhsT, rhs, start, stop, reads, writes, signal):
        self.op(self.PE, lambda e: e.matmul(out, lhsT, rhs, start=start, stop=stop), reads, writes, signal)

    def tr(self, out, in_, ident, reads, writes, signal):
        self.op(self.PE, lambda e: e.transpose(out, in_, ident), reads, writes, signal)

    def act(self, out, in_, func, reads, writes, bias=None, scale=None, accum_out=None):
        kw = {}
        if bias is not None:
            kw["bias"] = bias
        if scale is not None:
            kw["scale"] = scale
        if accum_out is not None:
            kw["accum_out"] = accum_out
        self.op(self.ACT, lambda e: e.activation(out, in_, func, **kw), reads, writes)

    def tt(self, E, out, in0, in1, op, reads, writes):
        self.op(E, lambda e: e.tensor_tensor(out, in0, in1, op), reads, writes)

    def ts(self, E, out, in0, s1, s2, op0, op1, reads, writes):
        if op1 is None:
            self.op(E, lambda e: e.tensor_scalar(out, in0, s1, None, op0), reads, writes)
        else:
            self.op(E, lambda e: e.tensor_scalar(out, in0, s1, s2, op0, op1), reads, writes)

    def stt(self, out, in0, scalar, in1, op0, op1, reads, writes):
        self.op(self.DVE, lambda e: e.scalar_tensor_tensor(out, in0, scalar, in1, op0, op1), reads, writes)

    def cp(self, E, out, in_, reads, writes):
        if E is self.ACT:
            self.op(E, lambda e: e.copy(out, in_), reads, writes)
        else:
            self.op(E, lambda e: e.tensor_copy(out, in_), reads, writes)


class Ctx:
    def __init__(self, nc, es):
        self.nc, self.es, self.n = nc, es, 0

    def sb(self, shape, dt, name=None):
        self.n += 1
        t = self.es.enter_context(self.nc.sbuf_tensor("%s_%d" % (name or "t", self.n), list(shape), dt))
        return t, Buf(name or "t")

    def ps(self, shape, dt, name=None):
        self.n += 1
        t = self.es.enter_context(self.nc.psum_tensor("%s_%d" % (name or "p", self.n), list(shape), dt))
        return t, Buf(name or "p")


def dram_in(nc, name, shape, dt):
    return nc.dram_tensor(name, list(shape), dt, kind="ExternalInput")


def dram_out(nc, name, shape, dt):
    return nc.dram_tensor(name, list(shape), dt, kind="ExternalOutput")


class NormT:
    def __init__(self, kb, cx, ident, identB):
        self.kb = kb
        self.ident, self.identB = ident, identB
        self.sets = []
        for i in range(2):
            sq, sqB = cx.sb([128, D], BF16, "nsq")
            ss, ssB = cx.sb([128, 1], F32, "nss")
            rs, rsB = cx.sb([128, 1], F32, "nrs")
            xs, xsB = cx.sb([128, D], BF16, "nxs")
            tp, tpB = cx.ps([128, 8, 128], BF16, "ntp")
            self.sets.append((sq, sqB, ss, ssB, rs, rsB, xs, xsB, tp, tpB))
        self.k = 0

    def run(self, x_ap, xB, gT_ap, gB_, out_ap, outB, eps=1e-6):
        kb = self.kb
        sq, sqB, ss, ssB, rs, rsB, xs, xsB, tp, tpB = self.sets[self.k % 2]
        self.k += 1
        kb.act(sq[:, :], x_ap, AF.Square, [xB], [sqB, ssB], accum_out=ss[:, 0:1])
        kb.ts(kb.DVE, rs[:, :], ss[:, :], 1.0 / D, eps, ALU.mult, ALU.add, [ssB], [rsB])
        kb.act(rs[:, :], rs[:, :], AF.Sqrt, [rsB], [rsB])
        kb.op(kb.DVE, lambda e: e.reciprocal(rs[:, :], rs[:, :]), [rsB], [rsB])
        kb.act(xs[:, :], x_ap, AF.Copy, [xB, rsB], [xsB], scale=rs[:, 0:1])
        for j in range(8):
            kb.tr(tp[:, j, :], xs[:, j * 128:(j + 1) * 128], self.ident[:, :], [xsB, self.identB], [tpB], j == 7)
        kb.tt(kb.DVE, out_ap, tp[:, :, :], gT_ap.unsqueeze(2).broadcast_to([128, 8, 128]), ALU.mult,
              [tpB, gB_], [outB])


def post_norm_residual(kb, banks, X_ap, XB, gB_ap, gBB, scr, eps=1e-6):
    sq, sqB, ssa, ssaB, ssb, ssbB, rs, rsB, tmp, tmpB = scr
    (p0, p0B), (p1, p1B) = banks
    kb.act(sq[:, 0:512], p0, AF.Square, [p0B], [sqB, ssaB], accum_out=ssa[:, 0:1])
    kb.act(sq[:, 512:1024], p1, AF.Square, [p1B], [sqB, ssbB], accum_out=ssb[:, 0:1])
    kb.tt(kb.DVE, rs[:, :], ssa[:, :], ssb[:, :], ALU.add, [ssaB, ssbB], [rsB])
    kb.ts(kb.DVE, rs[:, :], rs[:, :], 1.0 / D, eps, ALU.mult, ALU.add, [rsB], [rsB])
    kb.act(rs[:, :], rs[:, :], AF.Sqrt, [rsB], [rsB])
    kb.op(kb.DVE, lambda e: e.reciprocal(rs[:, :], rs[:, :]), [rsB], [rsB])
    kb.stt(tmp[:, 0:512], p0, rs[:, 0:1], gB_ap[:, 0:512], ALU.mult, ALU.mult, [p0B, rsB, gBB], [tmpB])
    kb.stt(tmp[:, 512:1024], p1, rs[:, 0:1], gB_ap[:, 512:1024], ALU.mult, ALU.mult, [p1B, rsB, gBB], [tmpB])
    kb.tt(kb.POOL, X_ap, X_ap, tmp[:, :], ALU.add, [tmpB, XB], [XB])


def pn_scratch(cx):
    sq, sqB = cx.sb([128, D], BF16, "psq")
    ssa, ssaB = cx.sb([128, 1], F32, "pssa")
    ssb, ssbB = cx.sb([128, 1], F32, "pssb")
    rs, rsB = cx.sb([128, 1], F32, "prs")
    tmp, tmpB = cx.sb([128, D], F32, "ptmp")
    return (sq, sqB, ssa, ssaB, ssb, ssbB, rs, rsB, tmp, tmpB)


def build_cast(nrows_per_core):
    nc = bass.Bass("TRN2", target_bir_lowering=False)
    R = nrows_per_core
    assert R % 128 == 0
    w = dram_in(nc, "w", [R, 1024], F32)
    o = dram_out(nc, "o", [R, 1024], BF16)
    with ExitStack() as es:
        kb = KB(nc, es)
        cx = Ctx(nc, es)
        st = [cx.sb([128, 4, 1024], F32, "st") for _ in range(3)]
        ob = [cx.sb([128, 4, 1024], BF16, "ob") for _ in range(3)]
        nt = R // 128
        i = 0
        k = 0
        engs = [kb.DVE, kb.POOL, kb.ACT]
        while i < nt:
            n = min(4, nt - i)
            s, sB = st[k % 3]
            b, bB = ob[k % 3]
            src = w[i * 128:(i + n) * 128, :].rearrange("(a p) c -> p a c", p=128)
            dst = o[i * 128:(i + n) * 128, :].rearrange("(a p) c -> p a c", p=128)
            kb.dma(kb.SP, s[:, 0:n, :], src, [], [sB])
            for a in range(n):
                kb.cp(engs[(k * 4 + a) % 3], b[:, a, :], s[:, a, :], [sB], [bB])
            kb.dma(kb.SP, dst, b[:, 0:n, :], [bB], [])
            i += n
            k += 1
        kb.flush(final=True)
    return nc


def build_p1():
    nc = bass.Bass("TRN2", target_bir_lowering=False)
    x = dram_in(nc, "x", [TOK, D], F32)
    xh = dram_in(nc, "xh", [128, D], F32)
    w_in = dram_in(nc, "w_in", [D, 3072], BF16)
    gT_d = dram_in(nc, "gT", [128, 8], F32)
    cw_d = dram_in(nc, "cw", [128, 12], F32)
    id_d = dram_in(nc, "ident", [128, 128], BF16)
    QT_o = dram_out(nc, "QT", [512, TOK], BF16)
    KT_o = dram_out(nc, "KT", [512, TOK], BF16)
    V_o = dram_out(nc, "V", [TOK, 512], BF16)
    BO_o = dram_out(nc, "BO", [512, TOK], BF16)
    with ExitStack() as es:
        kb = KB(nc, es)
        cx = Ctx(nc, es)
        W, WB = cx.sb([128, 8, 3072], BF16, "W")
        gT, gTB = cx.sb([128, 8], F32, "gT")
        cw, cwB = cx.sb([128, 12], F32, "cw")
        ident, identB = cx.sb([128, 128], BF16, "id")
        kb.dma(kb.SP, gT[:, :], gT_d[:, :], [], [gTB])
        kb.dma(kb.SP, cw[:, :], cw_d[:, :], [], [cwB])
        kb.dma(kb.SP, ident[:, :], id_d[:, :], [], [identB])
        for p in range(6):
            kb.dma(kb.SP, W[:, :, p * 512:(p + 1) * 512],
                   w_in[:, p * 512:(p + 1) * 512].rearrange("(j p) c -> p j c", p=128), [], [WB])
        nt = NormT(kb, cx, ident, identB)
        xt = [cx.sb([128, D], F32, "xt") for _ in range(3)]
        hT = [cx.sb([128, 8, 512], BF16, "hT") for _ in range(2)]
        mmb = [cx.ps([128, 512], F32, "mm") for _ in range(6)]
        mmk = [0]

        def bank():
            b = mmb[mmk[0] % 6]
            mmk[0] += 1
            return b
        stg = [cx.sb([128, 512], BF16, "stg") for _ in range(4)]
        stk = [0]
        z = [cx.sb([128, 514], F32, "z") for _ in range(4)]
        cgs = [cx.sb([128, 512], F32, "cgs") for _ in range(2)]
        yb = [cx.sb([128, 512], F32, "y") for _ in range(2)]
        bo = [cx.sb([128, 512], BF16, "bo") for _ in range(2)]
        evk = [0]

        def evac(dst, dstB, src, srcB):
            e = kb.ACT if evk[0] % 2 == 0 else kb.DVE
            evk[0] += 1
            kb.cp(e, dst, src, [srcB], [dstB])

        def proj_fm(col0, hT_ap, hTB, n):
            pb, pbB = bank()
            for j in range(8):
                kb.mm(pb[:, 0:n], W[:, j, col0:col0 + 128], hT_ap[:, j, 0:n], j == 0, j == 7,
                      [WB, hTB], [pbB], j == 7)
            return pb, pbB

        xa, xaB = xt[0]
        kb.dma(kb.SP, xa[:, :], xh[:, :], [], [xaB])
        hh, hhB = hT[1]
        nt.run(xa[:, :], xaB, gT[:, :], gTB, hh[:, :, 0:128], hhB)
        for cb in range(4):
            cg, cgB = proj_fm(2048 + cb * 128, hh, hhB, 128)
            hc, hcB = proj_fm(2560 + cb * 128, hh, hhB, 128)
            c_, c_B = cgs[cb % 2]
            kb.cp(kb.ACT, c_[:, 0:128], cg[:, 0:128], [cgB], [c_B])
            y_, y_B = yb[cb % 2]
            kb.tt(kb.DVE, y_[:, 0:128], c_[:, 0:128], hc[:, 0:128], ALU.mult, [c_B, hcB], [y_B])
            zz, zzB = z[cb]
            kb.cp(kb.POOL, zz[:, 0:2], y_[:, 126:128], [y_B], [zzB])

        xk = 1
        for ci in range(4):
            h_, h_B = hT[ci % 2]
            for ti in range(4):
                t = ci * 4 + ti
                xa, xaB = xt[xk % 3]
                xk += 1
                kb.dma(kb.SP, xa[:, :], x[t * 128:(t + 1) * 128, :], [], [xaB])
                nt.run(xa[:, :], xaB, gT[:, :], gTB, h_[:, :, ti * 128:(ti + 1) * 128], h_B)
            tsl = slice(ci * 512, (ci + 1) * 512)
            for blk in range(8):
                pb, pbB = proj_fm(blk * 128, h_, h_B, 512)
                s_, s_B = stg[stk[0] % 4]
                stk[0] += 1
                evac(s_[:, :], s_B, pb[:, :], pbB)
                dst = (QT_o if blk < 4 else KT_o)[(blk % 4) * 128:(blk % 4 + 1) * 128, tsl]
                kb.dma(kb.POOL, dst, s_[:, :], [s_B], [])
            for ti in range(4):
                pb, pbB = bank()
                for j in range(8):
                    kb.mm(pb[:, :], h_[:, j, ti * 128:(ti + 1) * 128], W[:, j, 1024:1536], j == 0, j == 7,
                          [WB, h_B], [pbB], j == 7)
                s_, s_B = stg[stk[0] % 4]
                stk[0] += 1
                evac(s_[:, :], s_B, pb[:, :], pbB)
                t = ci * 4 + ti
                kb.dma(kb.POOL, V_o[t * 128:(t + 1) * 128, :], s_[:, :], [s_B], [])
            for cb in range(4):
                bg, bgB = proj_fm(1536 + cb * 128, h_, h_B, 512)
                cg, cgB = proj_fm(2048 + cb * 128, h_, h_B, 512)
                hc, hcB = proj_fm(2560 + cb * 128, h_, h_B, 512)
                c_, c_B = cgs[cb % 2]
                kb.cp(kb.ACT, c_[:, :], cg[:, :], [cgB], [c_B])
                zz, zzB = z[cb]
                kb.tt(kb.DVE, zz[:, 2:514], c_[:, :], hc[:, :], ALU.mult, [c_B, hcB], [zzB])
                y_, y_B = yb[cb % 2]
                kb.ts(kb.DVE, y_[:, :], zz[:, 2:514], cw[:, 8 + cb:9 + cb], None, ALU.mult, None, [zzB, cwB], [y_B])
                kb.stt(y_[:, :], zz[:, 1:513], cw[:, 4 + cb:5 + cb], y_[:, :], ALU.mult, ALU.add, [zzB, cwB, y_B], [y_B])
                kb.stt(y_[:, :], zz[:, 0:512], cw[:, cb:cb + 1], y_[:, :], ALU.mult, ALU.add, [zzB, cwB, y_B], [y_B])
                b_, b_B = bo[cb % 2]
                kb.tt(kb.DVE, b_[:, :], y_[:, :], bg[:, :], ALU.mult, [y_B, bgB], [b_B])
                kb.cp(kb.POOL, zz[:, 0:2], zz[:, 512:514], [zzB], [zzB])
                kb.dma(kb.POOL, BO_o[cb * 128:(cb + 1) * 128, tsl], b_[:, :], [b_B], [])
        kb.flush(final=True)
    return nc


def build_p2():
    nc = bass.Bass("TRN2", target_bir_lowering=False)
    QT_d = dram_in(nc, "QT", [128, SEQ], BF16)
    KT_d = dram_in(nc, "KT", [128, SEQ], BF16)
    V_d = dram_in(nc, "V", [SEQ, 128], BF16)
    bg_d = dram_in(nc, "biasg", [128, 1152], F32)
    mk_d = dram_in(nc, "maskc", [128, 1152], F32)
    c31_d = dram_in(nc, "c31", [128, 1], F32)
    lam_d = dram_in(nc, "lam", [128, 256], F32)
    sg_d = dram_in(nc, "subg", [128, 1], F32)
    ob_d = dram_in(nc, "ones_bf", [128, 128], BF16)
    of_d = dram_in(nc, "ones_f", [128, 128], F32)
    AT_o = dram_out(nc, "AT", [128, SEQ], BF16)
    with ExitStack() as es:
        kb = KB(nc, es)
        cx = Ctx(nc, es)
        QT, QTB = cx.sb([128, SEQ], BF16, "QT")
        KT, KTB = cx.sb([128, SEQ], BF16, "KT")
        Vb, VbB = cx.sb([128, 64, 128], BF16, "V")
        AT, ATB = cx.sb([128, SEQ], BF16, "AT")
        M, MB = cx.sb([128, 1152], F32, "M")
        mk, mkB = cx.sb([128, 1152], F32, "mk")
        c31, c31B = cx.sb([128, 1], F32, "c31")
        lam, lamB = cx.sb([128, 256], F32, "lam")
        subg, subgB = cx.sb([128, 1], F32, "subg")
        onesb, onesbB = cx.sb([128, 128], BF16, "onesb")
        onesf, onesfB = cx.sb([128, 128], F32, "onesf")
        sm, smB = cx.sb([128, 8], F32, "sm")
        lt, ltB = cx.sb([128, 64], F32, "lt")
        for a in range(4):
            sl = slice(a * 2048, (a + 1) * 2048)
            kb.dma(kb.SP, KT[:, sl], KT_d[:, sl], [], [KTB])
            kb.dma(kb.SP, QT[:, sl], QT_d[:, sl], [], [QTB])
            kb.dma(kb.POOL, Vb[:, a * 16:(a + 1) * 16, :],
                   V_d[a * 2048:(a + 1) * 2048, :].rearrange("(j p) e -> p j e", p=128), [], [VbB])
        kb.dma(kb.SP, M[:, :], bg_d[:, :], [], [MB])
        kb.dma(kb.SP, mk[:, :], mk_d[:, :], [], [mkB])
        kb.dma(kb.SP, c31[:, :], c31_d[:, :], [], [c31B])
        kb.dma(kb.SP, lam[:, :], lam_d[:, :], [], [lamB])
        kb.dma(kb.SP, subg[:, :], sg_d[:, :], [], [subgB])
        kb.dma(kb.SP, onesb[:, :], ob_d[:, :], [], [onesbB])
        kb.dma(kb.SP, onesf[:, :], of_d[:, :], [], [onesfB])
        kb.tt(kb.DVE, M[:, :], M[:, :], mk[:, :], ALU.add, [mkB, MB], [MB])
        kb.tt(kb.DVE, lt[:, :], lam[:, 0:64], lam[:, 64:128], ALU.mult, [lamB], [ltB])
        kb.op(kb.DVE, lambda e: e.tensor_reduce(sm[:, 0:1], lt[:, :], AX.X, ALU.add), [ltB], [smB])
        kb.tt(kb.DVE, lt[:, :], lam[:, 128:192], lam[:, 192:256], ALU.mult, [lamB, ltB], [ltB])
        kb.op(kb.DVE, lambda e: e.tensor_reduce(sm[:, 1:2], lt[:, :], AX.X, ALU.add), [ltB, smB], [smB])
        kb.act(sm[:, 3:5], sm[:, 0:2], AF.Exp, [smB], [smB])
        kb.tt(kb.DVE, sm[:, 2:3], sm[:, 4:5], sm[:, 3:4], ALU.subtract, [smB], [smB])
        kb.ts(kb.DVE, sm[:, 2:3], sm[:, 2:3], -0.2, None, ALU.add, None, [smB], [smB])
        kb.ts(kb.DVE, sm[:, 5:6], subg[:, :], 0.8, None, ALU.mult, None, [subgB, smB], [smB])

        Sb = [[cx.ps([128, 512], F32, "S%d" % m) for _ in range(2)] for m in range(2)]
        OT = [cx.ps([128, 512], F32, "OT") for _ in range(2)]
        SS = [cx.ps([128, 512], F32, "SS") for _ in range(2)]
        P = [[cx.sb([128, 512], BF16, "P%d" % m) for _ in range(3)] for m in range(2)]
        tS = [cx.sb([128, 512], F32, "tS") for _ in range(2)]
        r_ = [cx.sb([128, 512], F32, "r") for _ in range(2)]
        o_, o_B = cx.sb([128, 512], F32, "o")
        t1, t1B = cx.sb([128, 512], F32, "t1")
        o2, o2B = cx.sb([128, 512], F32, "o2")
        sq, sqB = cx.sb([128, 512], F32, "sq")
        rstd, rstdB = cx.sb([128, 512], F32, "rstd")

        pairs = [(c, j) for c in range(16) for j in range(4 * c + 4)]
        scount = [0]

        def emit_S(c, j):
            slot = scount[0] % 2
            scount[0] += 1
            Dk = 128 * j - 512 * c
            q0 = max(Dk, 0)
            qs = slice(c * 512 + q0, (c + 1) * 512)
            ks = slice(j * 128, (j + 1) * 128)
            for m in range(2):
                ps, psB = Sb[m][slot]
                rows = slice(m * 64, (m + 1) * 64)
                kb.mm(ps[:, q0:512], KT[rows, ks], QT[rows, qs], True, True, [KTB, QTB], [psB], True)
            return slot

        def emit_exp(c, j, slot, pslot):
            Dk = 128 * j - 512 * c
            q0 = max(Dk, 0)
            for m in range(2):
                ps, psB = Sb[m][slot]
                p_, p_B = P[m][pslot]
                if Dk < -128:
                    kb.act(p_[:, q0:512], ps[:, q0:512], AF.Exp, [psB, c31B], [p_B], bias=c31[:, 0:1], scale=0.125)
                else:
                    t_, t_B = tS[m]
                    kb.stt(t_[:, q0:512], ps[:, q0:512], 0.125, M[:, 512 - Dk + q0:1024 - Dk], ALU.mult, ALU.add,
                           [psB, MB], [t_B])
                    kb.act(p_[:, q0:512], t_[:, q0:512], AF.Exp, [t_B], [p_B])

        def emit_PV(c, j, pslot):
            Dk = 128 * j - 512 * c
            q0 = max(Dk, 0)
            first = (j == 0)
            last = (j == 4 * c + 3)
            for m in range(2):
                p_, p_B = P[m][pslot]
                ot, otB = OT[m]
                s2, s2B = SS[m]
                kb.mm(ot[:, q0:512], Vb[:, j, :], p_[:, q0:512], first, last, [VbB, p_B], [otB], last)
                kb.mm(s2[:, q0:512], onesb[:, :], p_[:, q0:512], first, last, [onesbB, p_B], [s2B], last)

        def epilogue(c):
            slot = scount[0] % 2
            scount[0] += 1
            for m in range(2):
                s2, s2B = SS[m]
                rr, rrB = r_[m]
                kb.op(kb.DVE, lambda e, rr=rr, s2=s2: e.reciprocal(rr[:, :], s2[:, :]), [s2B], [rrB])
            kb.tt(kb.DVE, o_[:, :], OT[0][0][:, :], r_[0][0][:, :], ALU.mult, [OT[0][1], r_[0][1]], [o_B])
            kb.tt(kb.DVE, t1[:, :], OT[1][0][:, :], r_[1][0][:, :], ALU.mult, [OT[1][1], r_[1][1]], [t1B])
            kb.stt(o2[:, :], t1[:, :], sm[:, 2:3], o_[:, :], ALU.mult, ALU.add, [t1B, smB, o_B], [o2B])
            kb.act(sq[:, :], o2[:, :], AF.Square, [o2B], [sqB])
            ps, psB = Sb[0][slot]
            kb.mm(ps[:, :], onesf[:, :], sq[:, :], True, True, [onesfB, sqB], [psB], True)
            kb.ts(kb.DVE, rstd[:, :], ps[:, :], 1.0 / 128, 1e-5, ALU.mult, ALU.add, [psB], [rstdB])
            kb.act(rstd[:, :], rstd[:, :], AF.Ln, [rstdB], [rstdB])
            kb.act(rstd[:, :], rstd[:, :], AF.Exp, [rstdB], [rstdB], scale=-0.5)
            kb.stt(AT[:, c * 512:(c + 1) * 512], o2[:, :], sm[:, 5:6], rstd[:, :], ALU.mult, ALU.mult,
                   [o2B, smB, rstdB], [ATB])
            kb.dma(kb.SP, AT_o[:, c * 512:(c + 1) * 512], AT[:, c * 512:(c + 1) * 512], [ATB], [])

        slots = {}
        slots[0] = emit_S(*pairs[0])
        for i, (c, j) in enumerate(pairs):
            nxt = pairs[i + 1] if i + 1 < len(pairs) else None
            if nxt is not None and nxt[0] == c:
                slots[i + 1] = emit_S(*nxt)
            emit_exp(c, j, slots[i], i % 3)
            emit_PV(c, j, i % 3)
            if j == 4 * c + 3:
                epilogue(c)
                if nxt is not None:
                    slots[i + 1] = emit_S(*nxt)
        kb.flush(final=True)
    return nc


def build_p3(debug=False):
    nc = bass.Bass("TRN2", target_bir_lowering=False)
    x = dram_in(nc, "x", [TOK, D], F32)
    AT_d = dram_in(nc, "AT", [512, TOK], BF16)
    BO_d = dram_in(nc, "BO", [512, TOK], BF16)
    wo_d = [dram_in(nc, "wo%d" % i, [D, D], BF16) for i in range(2)]
    wg_d = [dram_in(nc, "wg%d" % i, [D, FF], BF16) for i in range(2)]
    wu_d = [dram_in(nc, "wu%d" % i, [D, FF], BF16) for i in range(2)]
    wd_d = [dram_in(nc, "wd%d" % i, [FF, D], BF16) for i in range(2)]
    wi_d = dram_in(nc, "wi1", [D, 2048], BF16)
    gB_d = dram_in(nc, "gB", [128, 4, D], F32)
    gT_d = dram_in(nc, "gT", [128, 3, 8], F32)
    lng_d = dram_in(nc, "lng", [128, D], F32)
    lnb_d = dram_in(nc, "lnb", [128, D], F32)
    sw_d = dram_in(nc, "sguw", [128, 8, 128], F32)
    tr_d = dram_in(nc, "tril", [128, 128], F32)
    bB_d = dram_in(nc, "bB", [128, 8, 128], F32)
    id_d = dram_in(nc, "ident", [128, 128], BF16)
    y = dram_out(nc, "y", [TOK, D], F32)
    ydbg = [dram_out(nc, "y%d" % i, [TOK, D], F32) for i in (1, 2, 3)] if debug else None
    with ExitStack() as es:
        kb = KB(nc, es)
        cx = Ctx(nc, es)
        ident, identB = cx.sb([128, 128], BF16, "id")
        gT, gTB = cx.sb([128, 3, 8], F32, "gT")
        gB, gBB = cx.sb([128, D], F32, "gB")
        lng, lngB = cx.sb([128, D], F32, "lng")
        lnb, lnbB = cx.sb([128, D], F32, "lnb")
        bB, bBB = cx.sb([128, 8, 128], F32, "bB")
        sw, swB = cx.sb([128, 8, 128], F32, "sw")
        tril, trilB = cx.sb([128, 128], F32, "tril")
        swb, swbB = cx.sb([128, 8, 128], BF16, "swb")
        wsT, wsTB = cx.sb([128, 8, 128], BF16, "wsT")
        for dst, dB, src in ((ident, identB, id_d), (lng, lngB, lng_d), (lnb, lnbB, lnb_d), (tril, trilB, tr_d)):
            kb.dma(kb.SP, dst[:, :], src[:, :], [], [dB])
        for dst, dB, src in ((gT, gTB, gT_d), (bB, bBB, bB_d), (sw, swB, sw_d)):
            kb.dma(kb.SP, dst[:, :, :], src[:, :, :], [], [dB])

        X, XB = cx.sb([128, 4, D], F32, "X")
        hT, hTB = cx.sb([128, 8, 512], BF16, "hT")
        actT, actTB = cx.sb([128, NFB, 512], BF16, "actT")
        Wd, WdB = cx.sb([128, NFB, D], BF16, "Wd")
        Wo, WoB = cx.sb([128, 8, D], BF16, "Wo")
        pan = [[cx.sb([128, 8, 256], BF16, "pan") for _ in range(2)] for _ in range(2)]
        uT, uTB = cx.sb([128, 8, 512], BF16, "uT")
        ain, ainB = uT, uTB
        sg = [cx.sb([128, 512], F32, "sg") for _ in range(2)]
        vf, vfB = cx.sb([128, D], F32, "vf")
        vn, vnB = cx.sb([128, D], F32, "vn")
        vb, vbB = cx.sb([128, D], BF16, "vb")
        lsm, lsmB = cx.sb([128, 8], F32, "lsm")
        mxt, mxtB = cx.sb([128, 4, 128], F32, "mxt")
        nt = NormT(kb, cx, ident, identB)
        pns = pn_scratch(cx)
        mmb = [cx.ps([128, 512], F32, "mm") for _ in range(6)]
        mmk = [0]

        def bank():
            b = mmb[mmk[0] % 6]
            mmk[0] += 1
            return b

        kb.tt(kb.DVE, swb[:, :, :], sw[:, :, :], tril[:, :].unsqueeze(1).broadcast_to([128, 8, 128]), ALU.mult,
              [swB, trilB], [swbB])
        tp0, tp0B = nt.sets[0][8], nt.sets[0][9]
        for g in range(8):
            kb.tr(tp0[:, g, :], swb[:, g, :], ident[:, :], [swbB, identB], [tp0B], g == 7)
        kb.cp(kb.DVE, wsT[:, :, :], tp0[:, :, :], [tp0B], [wsTB])

        def ffn(li, gTi, gBi):
            for ti in range(4):
                nt.run(X[:, ti, :], XB, gT[:, gTi, :], gTB, hT[:, :, ti * 128:(ti + 1) * 128], hTB)
            for a in range(0, NFB, 2):
                kb.dma(kb.POOL, Wd[:, a:a + 2, :], wd_d[li][a * 128:(a + 2) * 128, :].rearrange("(f p) c -> p f c", p=128),
                       [], [WdB])
            npan = FF // 256
            def load(p):
                for gu, wsrc in ((0, wg_d[li]), (1, wu_d[li])):
                    w_, w_B = pan[gu][p % 2]
                    kb.dma(kb.SP, w_[:, :, :], wsrc[:, p * 256:(p + 1) * 256].rearrange("(j p) c -> p j c", p=128),
                           [], [w_B])
            load(0)
            for p in range(npan):
                if p + 1 < npan:
                    load(p + 1)
                wg_, wg_B = pan[0][p % 2]
                wu_, wu_B = pan[1][p % 2]
                for fb in range(2):
                    f = p * 2 + fb
                    gp, gpB = bank()
                    for j in range(8):
                        kb.mm(gp[:, :], wg_[:, j, fb * 128:(fb + 1) * 128], hT[:, j, :], j == 0, j == 7,
                              [wg_B, hTB], [gpB], j == 7)
                    up, upB = bank()
                    for j in range(8):
                        kb.mm(up[:, :], wu_[:, j, fb * 128:(fb + 1) * 128], hT[:, j, :], j == 0, j == 7,
                              [wu_B, hTB], [upB], j == 7)
                    s_, s_B = sg[f % 2]
                    kb.act(s_[:, :], gp[:, :], AF.Silu, [gpB], [s_B])
                    kb.tt(kb.DVE, actT[:, f, :], s_[:, :], up[:, :], ALU.mult, [s_B, upB], [actTB])
            kb.dma(kb.SP, gB[:, :], gB_d[:, gBi, :], [], [gBB])
            for ti in range(4):
                bk = [bank(), bank()]
                for hh in range(2):
                    pb, pbB = bk[hh]
                    for f in range(NFB):
                        kb.mm(pb[:, :], actT[:, f, ti * 128:(ti + 1) * 128], Wd[:, f, hh * 512:(hh + 1) * 512],
                              f == 0, f == NFB - 1, [actTB, WdB], [pbB], f == NFB - 1)
                post_norm_residual(kb, [(bk[0][0][:, :], bk[0][1]), (bk[1][0][:, :], bk[1][1])],
                                   X[:, ti, :], XB, gB[:, :], gBB, pns)

        def outproj(li, src, srcB, gBi):
            kb.dma(kb.SP, Wo[:, :, :], wo_d[li][:, :].rearrange("(j p) c -> p j c", p=128), [], [WoB])
            kb.dma(kb.SP, gB[:, :], gB_d[:, gBi, :], [], [gBB])
            for ti in range(4):
                bk = [bank(), bank()]
                for hh in range(2):
                    pb, pbB = bk[hh]
                    for j in range(8):
                        kb.mm(pb[:, :], src[:, j, ti * 128:(ti + 1) * 128], Wo[:, j, hh * 512:(hh + 1) * 512],
                              j == 0, j == 7, [srcB, WoB], [pbB], j == 7)
                post_norm_residual(kb, [(bk[0][0][:, :], bk[0][1]), (bk[1][0][:, :], bk[1][1])],
                                   X[:, ti, :], XB, gB[:, :], gBB, pns)

        def sgu2():
            for ti in range(4):
                nt.run(X[:, ti, :], XB, gT[:, 1, :], gTB, hT[:, :, ti * 128:(ti + 1) * 128], hTB)
            def load(p):
                w_, w_B = pan[p // 4 % 2][p % 2]
                kb.dma(kb.SP, w_[:, :, :], wi_d[:, p * 256:(p + 1) * 256].rearrange("(j p) c -> p j c", p=128), [], [w_B])
            load(0)
            for p in range(4):
                if p + 1 < 4:
                    load(p + 1)
                w_, w_B = pan[0][p % 2]
                for fb in range(2):
                    cblk = p * 2 + fb
                    pb, pbB = bank()
                    for j in range(8):
                        kb.mm(pb[:, :], w_[:, j, fb * 128:(fb + 1) * 128], hT[:, j, :], j == 0, j == 7,
                              [w_B, hTB], [pbB], j == 7)
                    kb.act(uT[:, cblk, :], pb[:, :], AF.Gelu, [pbB], [uTB])
            kb.dma(kb.SP, Wo[:, :, :], wi_d[:, 1024:2048].rearrange("(j p) c -> p j c", p=128), [], [WoB])
            for ti in range(4):
                bk = [bank(), bank()]
                for hh in range(2):
                    pb, pbB = bk[hh]
                    for j in range(8):
                        kb.mm(pb[:, :], hT[:, j, ti * 128:(ti + 1) * 128], Wo[:, j, hh * 512:(hh + 1) * 512],
                              j == 0, j == 7, [hTB, WoB], [pbB], j == 7)
                    kb.act(vf[:, hh * 512:(hh + 1) * 512], pb[:, :], AF.Gelu, [pbB], [vfB, lsmB],
                           accum_out=lsm[:, hh:hh + 1])
                    kb.act(vn[:, hh * 512:(hh + 1) * 512], vf[:, hh * 512:(hh + 1) * 512], AF.Square, [vfB], [vnB, lsmB],
                           accum_out=lsm[:, 2 + hh:3 + hh])
                kb.tt(kb.DVE, lsm[:, 4:5], lsm[:, 0:1], lsm[:, 1:2], ALU.add, [lsmB], [lsmB])
                kb.tt(kb.DVE, lsm[:, 5:6], lsm[:, 2:3], lsm[:, 3:4], ALU.add, [lsmB], [lsmB])
                kb.ts(kb.DVE, lsm[:, 4:6], lsm[:, 4:6], 1.0 / D, None, ALU.mult, None, [lsmB], [lsmB])
                kb.tt(kb.DVE, lsm[:, 6:7], lsm[:, 4:5], lsm[:, 4:5], ALU.mult, [lsmB], [lsmB])
                kb.tt(kb.DVE, lsm[:, 5:6], lsm[:, 5:6], lsm[:, 6:7], ALU.subtract, [lsmB], [lsmB])
                kb.ts(kb.DVE, lsm[:, 5:6], lsm[:, 5:6], 1e-5, None, ALU.add, None, [lsmB], [lsmB])
                kb.act(lsm[:, 5:6], lsm[:, 5:6], AF.Sqrt, [lsmB], [lsmB])
                kb.op(kb.DVE, lambda e: e.reciprocal(lsm[:, 5:6], lsm[:, 5:6]), [lsmB], [lsmB])
                kb.stt(lsm[:, 6:7], lsm[:, 4:5], -1.0, lsm[:, 5:6], ALU.mult, ALU.mult, [lsmB], [lsmB])
                kb.act(vn[:, :], vf[:, :], AF.Identity, [vfB, lsmB], [vnB], bias=lsm[:, 6:7], scale=lsm[:, 5:6])
                kb.tt(kb.DVE, vn[:, :], vn[:, :], lng[:, :], ALU.mult, [vnB, lngB], [vnB])
                kb.tt(kb.POOL, vb[:, :], vn[:, :], lnb[:, :], ALU.add, [vnB, lnbB], [vbB])
                for gh in range(2):
                    pb, pbB = bank()
                    for gg in range(4):
                        g = gh * 4 + gg
                        kb.mm(pb[:, gg * 128:(gg + 1) * 128], vb[:, g * 128:(g + 1) * 128], wsT[:, g, :], True, True,
                              [vbB, wsTB], [pbB], gg == 3)
                    kb.tt(kb.DVE, mxt[:, :, :], pb[:, :].rearrange("p (g t) -> p g t", g=4), bB[:, gh * 4:(gh + 1) * 4, :],
                          ALU.add, [pbB, bBB], [mxtB])
                    kb.tt(kb.DVE, uT[:, gh * 4:(gh + 1) * 4, ti * 128:(ti + 1) * 128], mxt[:, :, :],
                          uT[:, gh * 4:(gh + 1) * 4, ti * 128:(ti + 1) * 128], ALU.mult, [mxtB, uTB], [uTB])

        for ci in range(4):
            tsl = slice(ci * 512, (ci + 1) * 512)
            kb.dma(kb.SP, X[:, :, :], x[tsl, :].rearrange("(t p) d -> p t d", p=128), [], [XB])
            kb.dma(kb.SP, ain[:, 0:4, :], AT_d[:, tsl].rearrange("(j p) t -> p j t", p=128), [], [ainB])
            kb.dma(kb.SP, ain[:, 4:8, :], BO_d[:, tsl].rearrange("(j p) t -> p j t", p=128), [], [ainB])
            def snap(i):
                if debug:
                    kb.dma(kb.SP, ydbg[i][tsl, :].rearrange("(t p) d -> p t d", p=128), X[:, :, :], [XB], [])
            outproj(0, ain, ainB, 0)
            snap(0)
            ffn(0, 0, 1)
            snap(1)
            sgu2()
            outproj(1, uT, uTB, 2)
            snap(2)
            ffn(1, 2, 3)
            kb.dma(kb.SP, y[tsl, :].rearrange("(t p) d -> p t d", p=128), X[:, :, :], [XB], [])
        kb.flush(final=True)
    return nc


def _t5_bucket(n):
    n = np.maximum(n, 0)
    max_exact = 16
    nf = np.maximum(n, 1).astype(np.float32)
    large = max_exact + (np.log(nf / np.float32(max_exact)) / np.float32(np.log(128 / max_exact)) * np.float32(32 - max_exact)).astype(np.int32)
    large = np.minimum(large, 31)
    return np.where(n < max_exact, n, large)


_CACHE = {}
_DBG = {}


def _get(name, fn):
    if name not in _CACHE:
        _CACHE[name] = fn()
    return _CACHE[name]


def _run(nc, in_maps):
    return run_bass_kernel_spmd(nc, in_maps, core_ids=list(range(NCORE))).results


def _cast_weights(mats):
    flat = np.concatenate([np.ascontiguousarray(m, dtype=np.float32).reshape(-1) for m in mats])
    rows = flat.size // 1024
    assert rows % (8 * 128) == 0
    per = rows // 8
    flat = flat.reshape(8, per, 1024)
    nc = _get(("cast", per), lambda: build_cast(per))
    res = _run(nc, [{"w": flat[c]} for c in range(8)])
    out = np.concatenate([np.asarray(res[c]["o"]).reshape(-1) for c in range(8)])
    outs = []
    off = 0
    for m in mats:
        outs.append(out[off:off + m.size].reshape(m.shape))
        off += m.size
    return outs


def kernel(x, rel_bias, w_in_even, diff_lambda, diff_subln_g, conv_w, w_out_even,
           w_in_odd, sgu_ln_g, sgu_ln_b, sgu_w, sgu_b, w_out_odd, norm_g,
           w_gate, w_up, w_down):
    f32 = np.float32
    x = np.asarray(x, f32)
    norm_g = np.asarray(norm_g, f32)
    mats = [np.asarray(w_in_even[0], f32), np.asarray(w_out_even[0], f32), np.asarray(w_in_odd[0], f32),
            np.asarray(w_out_odd[0], f32), np.asarray(w_gate[0], f32), np.asarray(w_up[0], f32),
            np.asarray(w_down[0], f32), np.asarray(w_gate[1], f32), np.asarray(w_up[1], f32),
            np.asarray(w_down[1], f32)]
    tot = sum(m.size for m in mats)
    pad = (-tot) % (8 * 128 * 1024)
    if pad:
        mats.append(np.zeros((pad,), f32))
    cw_ = _cast_weights(mats)
    wi0, wo0, wi1, wo1, wg0, wu0, wd0, wg1, wu1, wd1 = cw_[:10]

    ident = np.eye(128, dtype=f32).astype(NPBF)

    def fm8(v):
        return np.ascontiguousarray(np.asarray(v, f32).reshape(8, 128).T)

    cw = np.asarray(conv_w[0], f32)
    cwT = np.ascontiguousarray(cw.reshape(3, 4, 128).transpose(2, 0, 1).reshape(128, 12))
    in1 = []
    for c in range(NCORE):
        b, r = divmod(c, 4)
        xo = x[b, r * TOK:(r + 1) * TOK]
        xh = np.zeros((128, D), f32)
        if r > 0:
            xh[126:128] = x[b, r * TOK - 2:r * TOK]
        in1.append({"x": np.ascontiguousarray(xo), "xh": xh, "w_in": wi0, "gT": fm8(norm_g[0, 0]),
                    "cw": cwT, "ident": ident})
    nc1 = _get("p1", build_p1)
    r1 = _run(nc1, in1)
    QT = [np.asarray(r1[c]["QT"]) for c in range(NCORE)]
    KT = [np.asarray(r1[c]["KT"]) for c in range(NCORE)]
    V = [np.asarray(r1[c]["V"]) for c in range(NCORE)]
    BO = [np.asarray(r1[c]["BO"]) for c in range(NCORE)]

    ki = np.arange(128)[:, None]
    u = np.arange(1152)[None, :]
    n = u - 512 - ki
    bucket = _t5_bucket(n)
    maskc = np.where(n >= 0, 0.0, NEG).astype(f32)
    rb = np.asarray(rel_bias, f32)
    lam = np.ascontiguousarray(np.broadcast_to(np.asarray(diff_lambda[0], f32).reshape(1, 256), (128, 256)))
    subg = np.ascontiguousarray(np.asarray(diff_subln_g[0], f32).reshape(128, 1))
    in2 = []
    for c in range(NCORE):
        b, h = divmod(c, 4)
        hs = slice(h * 128, (h + 1) * 128)
        in2.append({
            "QT": np.ascontiguousarray(np.concatenate([QT[b * 4 + r][hs] for r in range(4)], axis=1)),
            "KT": np.ascontiguousarray(np.concatenate([KT[b * 4 + r][hs] for r in range(4)], axis=1)),
            "V": np.ascontiguousarray(np.concatenate([V[b * 4 + r][:, hs] for r in range(4)], axis=0)),
            "biasg": np.ascontiguousarray(rb[bucket, h]),
            "maskc": maskc,
            "c31": np.ascontiguousarray(np.broadcast_to(rb[31:32, h:h + 1], (128, 1))),
            "lam": lam, "subg": subg,
            "ones_bf": np.ones((128, 128), f32).astype(NPBF), "ones_f": np.ones((128, 128), f32),
        })
    nc2 = _get("p2", build_p2)
    r2 = _run(nc2, in2)
    AT = [np.asarray(r2[c]["AT"]) for c in range(NCORE)]
    _DBG.update(QT=QT, KT=KT, V=V, BO=BO, AT=AT)

    gB = np.ascontiguousarray(np.broadcast_to(
        np.stack([norm_g[0, 1], norm_g[0, 3], norm_g[1, 1], norm_g[1, 3]])[None], (128, 4, D)))
    gT = np.ascontiguousarray(np.stack([fm8(norm_g[0, 2]), fm8(norm_g[1, 0]), fm8(norm_g[1, 2])], axis=1))
    lng = np.ascontiguousarray(np.broadcast_to(np.asarray(sgu_ln_g[0], f32)[None], (128, D)))
    lnb = np.ascontiguousarray(np.broadcast_to(np.asarray(sgu_ln_b[0], f32)[None], (128, D)))
    sguw = np.ascontiguousarray(np.asarray(sgu_w[0], f32).transpose(1, 0, 2))
    tril = np.tril(np.ones((128, 128), f32))
    bB = np.ascontiguousarray(np.broadcast_to(np.asarray(sgu_b[0], f32)[None], (128, 8, 128)))
    in3 = []
    for c in range(NCORE):
        b, r = divmod(c, 4)
        ts_ = slice(r * TOK, (r + 1) * TOK)
        in3.append({
            "x": np.ascontiguousarray(x[b, ts_]),
            "AT": np.ascontiguousarray(np.concatenate([AT[b * 4 + h][:, ts_] for h in range(4)], axis=0)),
            "BO": BO[c],
            "wo0": wo0, "wo1": wo1, "wg0": wg0, "wg1": wg1, "wu0": wu0, "wu1": wu1, "wd0": wd0, "wd1": wd1,
            "wi1": wi1, "gB": gB, "gT": gT, "lng": lng, "lnb": lnb, "sguw": sguw, "tril": tril, "bB": bB,
            "ident": ident,
        })
    nc3 = _get("p3", build_p3)
    r3 = _run(nc3, in3)
    out = np.empty((2, SEQ, D), f32)
    for c in range(NCORE):
        b, r = divmod(c, 4)
        out[b, r * TOK:(r + 1) * TOK] = np.asarray(r3[c]["y"])
    return out
```
